# Optimizing a Trainium2 kernel written in Bass

```python
import math
import jax, jax.numpy as jnp
from jax import lax
import numpy as np

D_MODEL = 2048
BATCH = 2
SEQ = 4096
DEPTH = 2
DEC_BATCH = 4
DEC_SEQ = 4096
PAST_LEN = 128

MIX_W = D_MODEL
M_W = D_MODEL // 2
M_HEADS = 8
M_HEAD_DIM = M_W // M_HEADS
A_W = MIX_W - M_W
A_HEADS = 4
A_HEAD_DIM = A_W // (2 * A_HEADS)
A_QK_W = A_HEADS * 2 * A_HEAD_DIM
N_GATES = 4 * M_HEADS
D_FF = 5504
CONV_W = 3
CHUNK = 128
Q_BLOCK = 128
ROPE_THETA = 10000.0
EPS = 1e-6

OFF_QK_M = 0
OFF_V_M = OFF_QK_M + 2 * M_W
OFF_O_M = OFF_V_M + M_W
OFF_G_M = OFF_O_M + M_W
OFF_Q_A = OFF_G_M + N_GATES
OFF_K_A = OFF_Q_A + A_QK_W
OFF_V_A = OFF_K_A + A_QK_W
IN_COLS = OFF_V_A + A_W

kernel_name = "hybrid_mlstm_diffattn_macaron_encoder"


def _rms_norm(x, g):
    xf = x.astype(jnp.float32)
    y = xf * lax.rsqrt(jnp.mean(xf * xf, axis=-1, keepdims=True) + EPS)
    return (y * g.astype(jnp.float32)).astype(x.dtype)


def _swiglu(h, w_gate, w_up, w_down):
    return (jax.nn.silu(h @ w_gate) * (h @ w_up)) @ w_down


def _centred_conv(x, w):
    pad = CONV_W // 2
    S = x.shape[1]
    xp = jnp.pad(x, ((0, 0), (pad, pad), (0, 0)))
    y = xp[:, 0:S] * w[0]
    for j in range(1, CONV_W):
        y = y + xp[:, j:j + S] * w[j]
    return y


def _rope_tables(S, d):
    inv = 1.0 / (ROPE_THETA ** (jnp.arange(0, d, 2, dtype=jnp.float32) / d))
    ang = jnp.arange(S, dtype=jnp.float32)[:, None] * inv[None, :]
    emb = jnp.concatenate([ang, ang], axis=-1)
    return jnp.cos(emb), jnp.sin(emb)


def _apply_rope(x, cos, sin):
    c = cos[None, :, None, None, :]
    s = sin[None, :, None, None, :]
    x1, x2 = jnp.split(x, 2, axis=-1)
    rot = jnp.concatenate([-x2, x1], axis=-1)
    return x * c + rot * s


def _mlstm_chunkwise(q, k, v, ig, lf):
    B, S, H, d = q.shape
    nc = S // CHUNK
    def vec_chunks(a):
        return a.reshape(B, nc, CHUNK, H, d).transpose(1, 0, 3, 2, 4)
    def gate_chunks(a):
        return a.reshape(B, nc, CHUNK, H).transpose(1, 0, 3, 2)
    tril = jnp.tril(jnp.ones((CHUNK, CHUNK), dtype=bool))

    def step(carry, inp):
        C, n, m = carry
        qc, kc, vc, igc, lfc = inp
        b = jnp.cumsum(lfc, axis=-1)
        Dm = jnp.where(tril, b[..., :, None] - b[..., None, :] + igc[..., None, :], -jnp.inf)
        g = b + m[..., None]
        m_t = jnp.maximum(g, jnp.max(Dm, axis=-1))
        w_intra = jnp.exp(Dm - m_t[..., None])
        w_inter = jnp.exp(g - m_t)
        s = jnp.einsum('bhtd,bhsd->bhts', qc, kc) * w_intra
        num = jnp.einsum('bhts,bhsd->bhtd', s, vc) + w_inter[..., None] * jnp.einsum('bhvk,bhtk->bhtv', C, qc)
        den = jnp.sum(s, axis=-1) + w_inter * jnp.einsum('bhk,bhtk->bht', n, qc)
        h = num / jnp.maximum(jnp.abs(den), jnp.exp(-m_t))[..., None]
        bL = b[..., -1]
        a = bL[..., None] - b + igc
        m_new = jnp.maximum(bL + m, jnp.max(a, axis=-1))
        ws = jnp.exp(a - m_new[..., None])
        wc = jnp.exp(bL + m - m_new)
        C_new = wc[..., None, None] * C + jnp.einsum('bhs,bhsv,bhsk->bhvk', ws, vc, kc)
        n_new = wc[..., None] * n + jnp.einsum('bhs,bhsk->bhk', ws, kc)
        return (C_new, n_new, m_new), h

    init = (jnp.zeros((B, H, d, d), jnp.float32), jnp.zeros((B, H, d), jnp.float32), jnp.zeros((B, H), jnp.float32))
    _, hs = lax.scan(step, init, (vec_chunks(q), vec_chunks(k), vec_chunks(v), gate_chunks(ig), gate_chunks(lf)))
    return hs.transpose(1, 0, 3, 2, 4).reshape(B, S, H, d)


def _token_mix(h, w_in, conv_qk, b_gate, mlstm_g, lq1, lk1, lq2, lk2, subln_g, w_out, lambda_init):
    B, S, _ = h.shape
    f32 = jnp.float32
    z = h @ w_in
    qk_m, v_m, o_m, gates, q_a, k_a, v_a = jnp.split(z, [OFF_V_M, OFF_O_M, OFF_G_M, OFF_Q_A, OFF_K_A, OFF_V_A], axis=-1)

    qk = jax.nn.silu(_centred_conv(qk_m, conv_qk)).astype(f32)
    q_m, k_m = jnp.split(qk, 2, axis=-1)
    q_m = q_m.reshape(B, S, M_HEADS, M_HEAD_DIM)
    k_m = k_m.reshape(B, S, M_HEADS, M_HEAD_DIM) * (M_HEAD_DIM ** -0.5)
    v_m = v_m.astype(f32).reshape(B, S, M_HEADS, M_HEAD_DIM)
    gp = gates.astype(f32).reshape(B, S, 4, M_HEADS) + b_gate.astype(f32)[None, None]
    ig_f, lf_f = gp[:, :, 0], jax.nn.log_sigmoid(gp[:, :, 1])
    ig_b, lf_b = gp[:, :, 2], jax.nn.log_sigmoid(gp[:, :, 3])
    h_f = _mlstm_chunkwise(q_m, k_m, v_m, ig_f, lf_f)
    flip = lambda a: jnp.flip(a, axis=1)
    h_b = flip(_mlstm_chunkwise(flip(q_m), flip(k_m), flip(v_m), flip(ig_b), flip(lf_b)))
    hm = h_f + h_b
    hm = hm * lax.rsqrt(jnp.mean(hm * hm, axis=-1, keepdims=True) + EPS)
    hm = hm * mlstm_g.astype(f32).reshape(M_HEADS, M_HEAD_DIM)
    out_m = (jax.nn.sigmoid(o_m.astype(f32)) * hm.reshape(B, S, M_W))

    cos, sin = _rope_tables(S, A_HEAD_DIM)
    qa = _apply_rope(q_a.astype(f32).reshape(B, S, A_HEADS, 2, A_HEAD_DIM), cos, sin)
    ka = _apply_rope(k_a.astype(f32).reshape(B, S, A_HEADS, 2, A_HEAD_DIM), cos, sin)
    ka = ka.transpose(0, 2, 3, 1, 4)
    va = v_a.astype(f32).reshape(B, S, A_HEADS, 2 * A_HEAD_DIM).transpose(0, 2, 1, 3)
    lam = (jnp.exp(jnp.sum(lq1.astype(f32) * lk1.astype(f32))) - jnp.exp(jnp.sum(lq2.astype(f32) * lk2.astype(f32))) + lambda_init)
    scale = A_HEAD_DIM ** -0.5
    nq = S // Q_BLOCK
    qblocks = qa.transpose(0, 2, 3, 1, 4).reshape(B, A_HEADS, 2, nq, Q_BLOCK, A_HEAD_DIM).transpose(3, 0, 1, 2, 4, 5)

    def attend(qb):
        s = jnp.einsum('bhcqd,bhckd->bhcqk', qb, ka) * scale
        p = jax.nn.softmax(s, axis=-1)
        a = p[:, :, 0] - lam * p[:, :, 1]
        return jnp.einsum('bhqk,bhke->bhqe', a, va)

    oa = lax.map(attend, qblocks)
    oa = oa.transpose(1, 0, 3, 2, 4).reshape(B, S, A_HEADS, 2 * A_HEAD_DIM)
    oa = oa * lax.rsqrt(jnp.mean(oa * oa, axis=-1, keepdims=True) + EPS) * subln_g.astype(f32) * (1.0 - lambda_init)
    out_a = oa.reshape(B, S, A_W)

    mixed = jnp.concatenate([out_m, out_a], axis=-1).astype(h.dtype)
    return mixed @ w_out


def _trunk(x, p):
    for l in range(DEPTH):
        lambda_init = 0.8 - 0.6 * math.exp(-0.3 * l)
        x = x + 0.5 * _swiglu(_rms_norm(x, p['ffn1_norm'][l]), p['ffn1_w_gate'][l], p['ffn1_w_up'][l], p['ffn1_w_down'][l])
        x = x + _token_mix(_rms_norm(x, p['mix_norm'][l]), p['w_in'][l], p['conv_qk'][l], p['b_gate'][l], p['mlstm_norm'][l],
                           p['lambda_q1'][l], p['lambda_k1'][l], p['lambda_q2'][l], p['lambda_k2'][l], p['diff_subln'][l],
                           p['w_out'][l], lambda_init)
        x = x + 0.5 * _swiglu(_rms_norm(x, p['ffn2_norm'][l]), p['ffn2_w_gate'][l], p['ffn2_w_up'][l], p['ffn2_w_down'][l])
    return _rms_norm(x, p['final_norm'])


def setup_inputs(seed: int = 0) -> dict:
    key = jax.random.key(seed)
    ks = jax.random.split(key, 24)
    nrm = lambda k, shape, s: jax.random.normal(k, shape, jnp.float32) * s
    gain = lambda k, shape: 1.0 + 0.02 * jax.random.normal(k, shape, jnp.float32)
    f_bias = jnp.linspace(3.0, 6.0, M_HEADS, dtype=jnp.float32)
    bias_base = jnp.stack([jnp.zeros((M_HEADS,), jnp.float32), f_bias, jnp.zeros((M_HEADS,), jnp.float32), f_bias])
    return {
        'x_prompt': nrm(ks[0], (BATCH, SEQ, D_MODEL), 1.0),
        'x_sample': nrm(ks[1], (DEC_BATCH, DEC_SEQ, D_MODEL), 1.0),
        'ffn1_norm': gain(ks[2], (DEPTH, D_MODEL)),
        'ffn1_w_gate': nrm(ks[3], (DEPTH, D_MODEL, D_FF), D_MODEL ** -0.5),
        'ffn1_w_up': nrm(ks[4], (DEPTH, D_MODEL, D_FF), D_MODEL ** -0.5),
        'ffn1_w_down': nrm(ks[5], (DEPTH, D_FF, D_MODEL), D_FF ** -0.5),
        'mix_norm': gain(ks[6], (DEPTH, D_MODEL)),
        'w_in': nrm(ks[7], (DEPTH, D_MODEL, IN_COLS), D_MODEL ** -0.5),
        'conv_qk': nrm(ks[8], (DEPTH, CONV_W, 2 * M_W), CONV_W ** -0.5),
        'b_gate': bias_base[None] + nrm(ks[9], (DEPTH, 4, M_HEADS), 0.1),
        'mlstm_norm': gain(ks[10], (DEPTH, M_W)),
        'lambda_q1': nrm(ks[11], (DEPTH, A_HEAD_DIM), 0.1),
        'lambda_k1': nrm(ks[12], (DEPTH, A_HEAD_DIM), 0.1),
        'lambda_q2': nrm(ks[13], (DEPTH, A_HEAD_DIM), 0.1),
        'lambda_k2': nrm(ks[14], (DEPTH, A_HEAD_DIM), 0.1),
        'diff_subln': gain(ks[15], (DEPTH, 2 * A_HEAD_DIM)),
        'w_out': nrm(ks[16], (DEPTH, MIX_W, D_MODEL), MIX_W ** -0.5),
        'ffn2_norm': gain(ks[17], (DEPTH, D_MODEL)),
        'ffn2_w_gate': nrm(ks[18], (DEPTH, D_MODEL, D_FF), D_MODEL ** -0.5),
        'ffn2_w_up': nrm(ks[19], (DEPTH, D_MODEL, D_FF), D_MODEL ** -0.5),
        'ffn2_w_down': nrm(ks[20], (DEPTH, D_FF, D_MODEL), D_FF ** -0.5),
        'final_norm': gain(ks[21], (D_MODEL,)),
    }


def reference(x_prompt, x_sample, ffn1_norm, ffn1_w_gate, ffn1_w_up, ffn1_w_down, mix_norm, w_in, conv_qk, b_gate,
              mlstm_norm, lambda_q1, lambda_k1, lambda_q2, lambda_k2, diff_subln, w_out, ffn2_norm, ffn2_w_gate,
              ffn2_w_up, ffn2_w_down, final_norm):
    p = {
        'ffn1_norm': ffn1_norm, 'ffn1_w_gate': ffn1_w_gate, 'ffn1_w_up': ffn1_w_up, 'ffn1_w_down': ffn1_w_down,
        'mix_norm': mix_norm, 'w_in': w_in, 'conv_qk': conv_qk, 'b_gate': b_gate, 'mlstm_norm': mlstm_norm,
        'lambda_q1': lambda_q1, 'lambda_k1': lambda_k1, 'lambda_q2': lambda_q2, 'lambda_k2': lambda_k2,
        'diff_subln': diff_subln, 'w_out': w_out, 'ffn2_norm': ffn2_norm, 'ffn2_w_gate': ffn2_w_gate,
        'ffn2_w_up': ffn2_w_up, 'ffn2_w_down': ffn2_w_down, 'final_norm': final_norm,
    }
    y_prompt = _trunk(x_prompt, p)
    y_sample = _trunk(x_sample, p)
    return (y_prompt, y_sample)
```

```python
import numpy as np
from contextlib import ExitStack
import ml_dtypes
import concourse.bass as bass
import concourse.mybir as mybir
from concourse.bass_utils import run_bass_kernel_spmd

F32 = mybir.dt.float32
BF16 = mybir.dt.bfloat16
AF = mybir.ActivationFunctionType
ALU = mybir.AluOpType
AX = mybir.AxisListType

D = 2048
NDC = 16
DFF = 5504
NFC = 43
SEQ = 4096
DEPTH = 2
MH = 8
AH = 4
INC = 7200
EPS = 1e-6
N_CORES = 8
N_SEQ = 6


class Buf:
    __slots__ = ("ap", "w", "r", "pw")

    def __init__(self, ap):
        self.ap = ap
        self.w = None
        self.r = {}
        self.pw = False

    def __getitem__(self, key):
        return self.ap[key]


class DSem:
    def __init__(self, sem, key):
        self.sem = sem
        self.key = key
        self.count = 0
        self.bufs = []
        self.sw = False


class K:
    def __init__(self, nc):
        self.nc = nc
        self.eng = {"pe": nc.tensor, "act": nc.scalar, "dve": nc.vector, "pool": nc.gpsimd, "sp": nc.sync}
        self.sems = {}
        self.cur = {}
        self.cnt = {}
        self.known = {e: {} for e in self.eng}
        self.pend = {e: [] for e in self.eng}
        self.last = {}
        self.dsems = []
        self.free_dsems = []
        self.free_dsems_sw = []
        self.phase_dsems = []
        self.nsem = 0
        self.pe_keys = set()
        for e in ("pe", "act", "dve", "pool"):
            self._new_eng_sem(e)

    def uid(self):
        self._uid = getattr(self, "_uid", 0) + 1
        return f"_{self._uid}"

    def _alloc(self, name):
        h = self.nc.alloc_semaphore(name=name)
        key = self.nsem
        self.nsem += 1
        self.sems[key] = h
        return key

    def _new_eng_sem(self, e):
        self.cur[e] = self._alloc(f"s_{e}_{self.nsem}")
        if e == 'pe':
            self.pe_keys.add(self.cur[e])
        self.cnt[e] = 0

    def new_dsem(self, name="d", sw=False):
        free = self.free_dsems_sw if sw else self.free_dsems
        if free:
            d = free.pop(0)
            d.bufs = []
        else:
            key = self._alloc(f"{name}_{self.nsem}")
            d = DSem(self.sems[key], key)
            d.sw = sw
            self.dsems.append(d)
        self.phase_dsems.append(d)
        return d

    def end_phase(self):
        self.barrier()
        for d in self.phase_dsems:
            (self.free_dsems_sw if d.sw else self.free_dsems).append(d)
        self.phase_dsems = []

    def wait(self, e, *toks):
        for t in toks:
            if t is None:
                continue
            key, v = t
            if e == 'pe' and key in self.pe_keys:
                continue
            if self.known[e].get(key, 0) < v:
                self.eng[e].wait_ge(self.sems[key], v)
                self.known[e][key] = v

    def _deps(self, e, reads, writes):
        for b in reads:
            assert not b.pw, "read of buffer with pending (unsignalled) write"
            self.wait(e, b.w)
        for b in writes:
            assert not b.pw or True
            self.wait(e, b.w)
            for key, v in b.r.items():
                self.wait(e, (key, v))

    def _commit(self, e, tok, reads, writes):
        if tok is None:
            for b in reads:
                self.pend[e].append((b, "r"))
            for b in writes:
                b.pw = True
                self.pend[e].append((b, "w"))
            return
        allr = [(b, "r") for b in reads] + [(b, "w") for b in writes] + self.pend[e]
        self.pend[e] = []
        key, v = tok
        for b, m in allr:
            if m == "r":
                if b.r.get(key, 0) < v:
                    b.r[key] = v
            else:
                b.w = tok
                b.r = {}
                b.pw = False

    def op(self, e, fn, reads=(), writes=(), sig=True):
        self._deps(e, reads, writes)
        inst = fn(self.eng[e])
        tok = None
        if sig:
            if self.cnt[e] >= 30000:
                self._new_eng_sem(e)
            self.cnt[e] += 1
            inst.then_inc(self.sems[self.cur[e]], 1)
            tok = (self.cur[e], self.cnt[e])
            self.last[e] = tok
        self._commit(e, tok, reads, writes)
        return tok

    def dma(self, e, dsem, out, in_, reads=(), writes=()):
        self._deps(e, reads, writes)
        self.eng[e].dma_start(out=out, in_=in_).then_inc(dsem.sem, 16)
        dsem.count += 16
        assert dsem.count < 32000
        tok = (dsem.key, dsem.count)
        key, v = tok
        for b in reads:
            if b.r.get(key, 0) < v:
                b.r[key] = v
        keep = []
        for b2 in dsem.bufs:
            if b2.w is not None and b2.w[0] == key and not any(b2 is b for b in writes):
                b2.w = tok
                keep.append(b2)
        dsem.bufs = keep
        for b in writes:
            b.w = tok
            b.r = {}
            dsem.bufs.append(b)
        return tok

    def barrier(self):
        toks = []
        for e in ("pe", "act", "dve", "pool"):
            assert not self.pend[e], f"pending unsignalled ops on {e}"
            if e in self.last:
                toks.append(self.last[e])
        for d in self.dsems:
            if d.count:
                toks.append((d.key, d.count))
        for e in self.eng:
            self.wait(e, *toks)


def _ring(n):
    i = 0
    while True:
        yield i % n
        i += 1


def ffn_phase(k, ps, src, dst, gam_d, wg, wu, wd, S, consts):
    nc = k.nc
    u = k.uid()
    TT = 1024
    NT = S // TT
    ones = consts["ones"]
    with ExitStack() as es1:
        xn_t = es1.enter_context(nc.sbuf_tensor("f_xn" + u, [128, NDC * TT], BF16))
        h_t = es1.enter_context(nc.sbuf_tensor("f_h" + u, [128, NFC * TT], BF16))
        xs_t = es1.enter_context(nc.sbuf_tensor("f_xs" + u, [128, 3 * TT], F32))
        sq_t = es1.enter_context(nc.sbuf_tensor("f_sq" + u, [128, 2 * TT], BF16))
        rstd_t = es1.enter_context(nc.sbuf_tensor("f_rstd" + u, [128, TT], F32))
        wgu_t = es1.enter_context(nc.sbuf_tensor("f_wgu" + u, [128, 4 * D], BF16))
        wd_t = es1.enter_context(nc.sbuf_tensor("f_wd" + u, [128, 2 * NFC * 128], BF16))
        sg_t = es1.enter_context(nc.sbuf_tensor("f_sg" + u, [128, 3 * 512], F32))
        o_t = es1.enter_context(nc.sbuf_tensor("f_o" + u, [128, 2 * TT], F32))
        gam_t = es1.enter_context(nc.sbuf_tensor("f_gam" + u, [128, NDC], F32))
        xn = Buf(xn_t)
        xnc = [Buf(xn_t[:, c * TT:(c + 1) * TT]) for c in range(NDC)]
        hb = [[Buf(h_t[:, f * TT + s * 512: f * TT + (s + 1) * 512]) for s in range(2)] for f in range(NFC)]
        xs = [Buf(xs_t[:, i * TT:(i + 1) * TT]) for i in range(3)]
        sq = [Buf(sq_t[:, i * TT:(i + 1) * TT]) for i in range(2)]
        rstd = Buf(rstd_t[:, :])
        wgs = [Buf(wgu_t[:, (2 * i) * D:(2 * i + 1) * D]) for i in range(2)]
        wus = [Buf(wgu_t[:, (2 * i + 1) * D:(2 * i + 2) * D]) for i in range(2)]
        wds = [Buf(wd_t[:, i * NFC * 128:(i + 1) * NFC * 128]) for i in range(2)]
        sg = [Buf(sg_t[:, i * 512:(i + 1) * 512]) for i in range(3)]
        ob = [[Buf(o_t[:, i * TT + s * 512:i * TT + (s + 1) * 512]) for s in range(2)] for i in range(2)]
        gam = Buf(gam_t[:, :])
        d_xs = [k.new_dsem("xs") for _ in range(3)]
        d_wg = [k.new_dsem("wg", sw=True) for _ in range(2)]
        d_wd = [k.new_dsem("wd", sw=True) for _ in range(2)]
        d_o = [k.new_dsem("o") for _ in range(2)]
        d_g = k.new_dsem("g")
        k.dma("sp", d_g, gam.ap, gam_d, writes=[gam])
        xs_i = _ring(3)
        sq_i = _ring(2)
        sg_i = _ring(3)
        o_i = _ring(2)
        gu_cnt = [0]
        wd_cnt = [0]

        def load_gu(f):
            s = gu_cnt[0] % 2
            gu_cnt[0] += 1
            k.dma("pool", d_wg[s], wgs[s].ap, wg[f], writes=[wgs[s]])
            k.dma("pool", d_wg[s], wus[s].ap, wu[f], writes=[wus[s]])
            return s

        def load_wd(dc):
            s = wd_cnt[0] % 2
            wd_cnt[0] += 1
            k.dma("pool", d_wd[s], wds[s].ap, wd[dc], writes=[wds[s]])
            return s

        def load_x(c, t):
            i = next(xs_i)
            k.dma("sp", d_xs[i], xs[i].ap, src[c * 128:(c + 1) * 128, t * TT:(t + 1) * TT], writes=[xs[i]])
            return xs[i]

        gu_slots = {}
        wd_slots = {}

        def norm_units(t):
            boxes = [dict() for _ in range(NDC)]

            def uA(c):
                x = load_x(c, t)
                q = sq[next(sq_i)]
                boxes[c]["q"] = q
                k.op("act", lambda e: e.activation(out=q.ap, in_=x.ap, func=AF.Square), reads=[x], writes=[q])

            def uB(c):
                q = boxes[c]["q"]
                for s in range(2):
                    k.op("pe", lambda e: e.matmul(ps[s].ap, lhsT=ones.ap, rhs=q.ap[:, s * 512:(s + 1) * 512],
                                                  start=(c == 0), stop=(c == NDC - 1)),
                         reads=[q, ones], writes=[ps[s]], sig=(c == NDC - 1 or s == 1))

            def uR():
                for s in range(2):
                    k.op("act", lambda e: e.activation(out=rstd.ap[:, s * 512:(s + 1) * 512], in_=ps[s].ap,
                                                       func=AF.Sqrt, bias=consts["eps"].ap, scale=1.0 / D),
                         reads=[ps[s], consts["eps"]], writes=[rstd])
                k.op("dve", lambda e: e.reciprocal(out=rstd.ap, in_=rstd.ap), reads=[rstd], writes=[rstd])

            def uP(c):
                x = load_x(c, t)
                k.op("dve", lambda e: e.scalar_tensor_tensor(out=xnc[c].ap, in0=x.ap, scalar=gam.ap[:, c:c + 1],
                                                             in1=rstd.ap, op0=ALU.mult, op1=ALU.mult),
                     reads=[x, rstd, gam], writes=[xnc[c]])
            units = []
            for c in range(NDC):
                units.append(lambda c=c: uA(c))
                if c >= 1:
                    units.append(lambda c=c: uB(c - 1))
            units.append(lambda: uB(NDC - 1))
            units.append(uR)
            for c in range(NDC):
                units.append(lambda c=c: uP(c))
            return units

        for t in range(NT):
            if t == 0:
                gu_slots[(t, 0)] = load_gu(0)
                for un in norm_units(0):
                    un()
            for f in range(NFC):
                if f + 1 < NFC:
                    gu_slots[(t, f + 1)] = load_gu(f + 1)
                elif True:
                    wd_slots[(t, 0)] = load_wd(0)
                ws = gu_slots.pop((t, f))
                for s in range(2):
                    pg = ps[(2 * (2 * f + s)) % 8]
                    pu = ps[(2 * (2 * f + s) + 1) % 8]
                    for c in range(NDC):
                        k.op("pe", lambda e: e.matmul(pg.ap, lhsT=wgs[ws].ap[:, c * 128:(c + 1) * 128],
                                                      rhs=xnc[c].ap[:, s * 512:(s + 1) * 512],
                                                      start=(c == 0), stop=(c == NDC - 1)),
                             reads=[wgs[ws], xnc[c]], writes=[pg], sig=(c == NDC - 1))
                    for c in range(NDC):
                        k.op("pe", lambda e: e.matmul(pu.ap, lhsT=wus[ws].ap[:, c * 128:(c + 1) * 128],
                                                      rhs=xnc[c].ap[:, s * 512:(s + 1) * 512],
                                                      start=(c == 0), stop=(c == NDC - 1)),
                             reads=[wus[ws], xnc[c]], writes=[pu], sig=(c == NDC - 1))
                    g = sg[next(sg_i)]
                    k.op("act", lambda e: e.activation(out=g.ap, in_=pg.ap, func=AF.Silu), reads=[pg], writes=[g])
                    k.op("dve", lambda e: e.tensor_tensor(out=hb[f][s].ap, in0=pu.ap, in1=g.ap, op=ALU.mult),
                         reads=[pu, g], writes=[hb[f][s]])
            for dc in range(NDC):
                if dc + 1 < NDC:
                    wd_slots[(t, dc + 1)] = load_wd(dc + 1)
                elif t + 1 < NT:
                    gu_slots[(t + 1, 0)] = load_gu(0)
                ws = wd_slots.pop((t, dc))
                oi = next(o_i)
                o = ob[oi]
                if dc == 0:
                    nxt_units = norm_units(t + 1) if t + 1 < NT else []
                    per_dc = (len(nxt_units) + NDC - 2) // (NDC - 1) if nxt_units else 0
                pbs = [ps[2 + (2 * dc + s_) % 6] for s_ in range(2)]
                for s in range(2):
                    p = pbs[s]
                    for f in range(NFC):
                        k.op("pe", lambda e: e.matmul(p.ap, lhsT=wds[ws].ap[:, f * 128:(f + 1) * 128],
                                                      rhs=hb[f][s].ap, start=(f == 0), stop=(f == NFC - 1)),
                             reads=[wds[ws], hb[f][s]], writes=[p], sig=(f == NFC - 1))
                        if s == 0 and f in (5, 15, 25, 35):
                            for _ in range((per_dc + 3) // 4):
                                if nxt_units:
                                    nxt_units.pop(0)()
                x = load_x(dc, t)
                for s in range(2):
                    p = pbs[s]
                    k.op("dve", lambda e: e.scalar_tensor_tensor(out=o[s].ap, in0=p.ap,
                                                                 scalar=0.5, in1=x.ap[:, s * 512:(s + 1) * 512],
                                                                 op0=ALU.mult, op1=ALU.add),
                         reads=[p, x], writes=[o[s]])
                k.dma("sp", d_o[oi], dst[dc * 128:(dc + 1) * 128, t * TT:(t + 1) * TT],
                      o_t[:, oi * TT:(oi + 1) * TT], reads=[o[0], o[1]])
            while nxt_units:
                nxt_units.pop(0)()
        k.end_phase()


def final_norm_phase(k, ps, src, dst, gam_d, S, consts):
    nc = k.nc
    u = k.uid()
    TT = 1024
    NT = S // TT
    ones = consts["ones"]
    with ExitStack() as es2:
        xs_t = es2.enter_context(nc.sbuf_tensor("n_xs" + u, [128, 4 * TT], F32))
        sq_t = es2.enter_context(nc.sbuf_tensor("n_sq" + u, [128, 2 * TT], BF16))
        rstd_t = es2.enter_context(nc.sbuf_tensor("n_rstd" + u, [128, 2 * TT], F32))
        o_t = es2.enter_context(nc.sbuf_tensor("n_o" + u, [128, 2 * TT], F32))
        gam_t = es2.enter_context(nc.sbuf_tensor("n_gam" + u, [128, NDC], F32))
        xs = [Buf(xs_t[:, i * TT:(i + 1) * TT]) for i in range(4)]
        sq = [Buf(sq_t[:, i * TT:(i + 1) * TT]) for i in range(2)]
        rstds = [Buf(rstd_t[:, i * TT:(i + 1) * TT]) for i in range(2)]
        ob = [Buf(o_t[:, i * TT:(i + 1) * TT]) for i in range(2)]
        gam = Buf(gam_t[:, :])
        d_xs = [k.new_dsem("nxs") for _ in range(4)]
        d_o = [k.new_dsem("no") for _ in range(2)]
        d_g = k.new_dsem("ng")
        k.dma("sp", d_g, gam.ap, gam_d, writes=[gam])
        xs_i = _ring(4)
        sq_i = _ring(2)
        o_i = _ring(2)

        def load_x(c, t):
            i = next(xs_i)
            k.dma("sp", d_xs[i], xs[i].ap, src[c * 128:(c + 1) * 128, t * TT:(t + 1) * TT], writes=[xs[i]])
            return xs[i]

        def stats_units(t):
            rstd = rstds[t % 2]
            boxes = [dict() for _ in range(NDC)]

            def uA(c):
                x = load_x(c, t)
                q = sq[next(sq_i)]
                boxes[c]["q"] = q
                k.op("act", lambda e: e.activation(out=q.ap, in_=x.ap, func=AF.Square), reads=[x], writes=[q])

            def uB(c):
                q = boxes[c]["q"]
                for s in range(2):
                    k.op("pe", lambda e: e.matmul(ps[s].ap, lhsT=ones.ap, rhs=q.ap[:, s * 512:(s + 1) * 512],
                                                  start=(c == 0), stop=(c == NDC - 1)),
                         reads=[q, ones], writes=[ps[s]], sig=(c == NDC - 1 or s == 1))

            def uR():
                for s in range(2):
                    k.op("act", lambda e: e.activation(out=rstd.ap[:, s * 512:(s + 1) * 512], in_=ps[s].ap,
                                                       func=AF.Sqrt, bias=consts["eps"].ap, scale=1.0 / D),
                         reads=[ps[s], consts["eps"]], writes=[rstd])
                k.op("dve", lambda e: e.reciprocal(out=rstd.ap, in_=rstd.ap), reads=[rstd], writes=[rstd])
            units = []
            for c in range(NDC):
                units.append(lambda c=c: uA(c))
                if c >= 1:
                    units.append(lambda c=c: uB(c - 1))
            units.append(lambda: uB(NDC - 1))
            units.append(uR)
            return units

        def out_units(t):
            rstd = rstds[t % 2]

            def uP(c):
                x = load_x(c, t)
                oi = next(o_i)
                o = ob[oi]
                k.op("dve", lambda e: e.scalar_tensor_tensor(out=o.ap, in0=x.ap, scalar=gam.ap[:, c:c + 1],
                                                             in1=rstd.ap, op0=ALU.mult, op1=ALU.mult),
                     reads=[x, rstd, gam], writes=[o])
                k.dma("sp", d_o[oi], dst[c * 128:(c + 1) * 128, t * TT:(t + 1) * TT], o.ap, reads=[o])
            return [(lambda c=c: uP(c)) for c in range(NDC)]

        for un in stats_units(0):
            un()
        for t in range(NT):
            nxt = stats_units(t + 1) if t + 1 < NT else []
            for un in out_units(t):
                un()
                for _ in range(2):
                    if nxt:
                        nxt.pop(0)()
            while nxt:
                nxt.pop(0)()
        k.end_phase()


ZC_Q, ZC_K, ZC_V, ZC_O, ZC_QA, ZC_KA, ZC_VA, ZC_G = 0, 8, 16, 24, 32, 40, 48, 56
NZC = 57


def norm_stage(k, ps, src, t, TT, xs, d_xs, xs_i, sq, sq_i, rstd, gam, xnc, consts):
    ones = consts["ones"]

    def load_x(c):
        i = next(xs_i)
        k.dma("sp", d_xs[i], xs[i].ap, src[c * 128:(c + 1) * 128, t * TT:(t + 1) * TT], writes=[xs[i]])
        return xs[i]

    for c in range(NDC):
        x = load_x(c)
        q = sq[next(sq_i)]
        k.op("act", lambda e: e.activation(out=q.ap, in_=x.ap, func=AF.Square), reads=[x], writes=[q])
        for s in range(2):
            k.op("pe", lambda e: e.matmul(ps[s].ap, lhsT=ones.ap, rhs=q.ap[:, s * 512:(s + 1) * 512],
                                          start=(c == 0), stop=(c == NDC - 1)),
                 reads=[q, ones], writes=[ps[s]], sig=(c == NDC - 1 or s == 1))
    for s in range(2):
        k.op("act", lambda e: e.activation(out=rstd.ap[:, s * 512:(s + 1) * 512], in_=ps[s].ap,
                                           func=AF.Sqrt, bias=consts["eps"].ap, scale=1.0 / D),
             reads=[ps[s], consts["eps"]], writes=[rstd])
    k.op("dve", lambda e: e.reciprocal(out=rstd.ap, in_=rstd.ap), reads=[rstd], writes=[rstd])
    for c in range(NDC):
        x = load_x(c)
        k.op("dve", lambda e: e.scalar_tensor_tensor(out=xnc[c].ap, in0=x.ap, scalar=gam.ap[:, c:c + 1],
                                                     in1=rstd.ap, op0=ALU.mult, op1=ALU.mult),
             reads=[x, rstd, gam], writes=[xnc[c]])


def make_norm_units(k, ps, src, t, TT, load_x, sq, sq_i, rstd, gam, xnc, consts):
    ones = consts["ones"]
    boxes = [dict() for _ in range(NDC)]

    def uA(c):
        x = load_x(c, t)
        q = sq[next(sq_i)]
        boxes[c]["q"] = q
        k.op("act", lambda e: e.activation(out=q.ap, in_=x.ap, func=AF.Square), reads=[x], writes=[q])

    def uB(c):
        q = boxes[c]["q"]
        for s in range(2):
            k.op("pe", lambda e: e.matmul(ps[s].ap, lhsT=ones.ap, rhs=q.ap[:, s * 512:(s + 1) * 512],
                                          start=(c == 0), stop=(c == NDC - 1)),
                 reads=[q, ones], writes=[ps[s]], sig=(c == NDC - 1 or s == 1))

    def uR():
        for s in range(2):
            k.op("act", lambda e: e.activation(out=rstd.ap[:, s * 512:(s + 1) * 512], in_=ps[s].ap,
                                               func=AF.Sqrt, bias=consts["eps"].ap, scale=1.0 / D),
                 reads=[ps[s], consts["eps"]], writes=[rstd])
        k.op("dve", lambda e: e.reciprocal(out=rstd.ap, in_=rstd.ap), reads=[rstd], writes=[rstd])

    def uP(c):
        x = load_x(c, t)
        k.op("dve", lambda e: e.scalar_tensor_tensor(out=xnc[c].ap, in0=x.ap, scalar=gam.ap[:, c:c + 1],
                                                     in1=rstd.ap, op0=ALU.mult, op1=ALU.mult),
             reads=[x, rstd, gam], writes=[xnc[c]])
    units = []
    for c in range(NDC):
        units.append(lambda c=c: uA(c))
        if c >= 1:
            units.append(lambda c=c: uB(c - 1))
    units.append(lambda: uB(NDC - 1))
    units.append(uR)
    for c in range(NDC):
        units.append(lambda c=c: uP(c))
    return units


def linear_phase(k, ps, mode, src, gam_d, w, n_out, dst, S, consts, resid=None, stat_chunks=None, stat_dst=None):
    nc = k.nc
    u = k.uid()
    TT = 1024
    NT = S // TT
    with ExitStack() as es3:
        xn_t = es3.enter_context(nc.sbuf_tensor("l_xn" + u, [128, 2 * NDC * TT], BF16))
        xs_t = es3.enter_context(nc.sbuf_tensor("l_xs" + u, [128, 3 * TT], F32))
        sq_t = es3.enter_context(nc.sbuf_tensor("l_sq" + u, [128, 2 * TT], BF16))
        rstd_t = es3.enter_context(nc.sbuf_tensor("l_rstd" + u, [128, TT], F32))
        w_t = es3.enter_context(nc.sbuf_tensor("l_w" + u, [128, 3 * D], BF16))
        o_t = es3.enter_context(nc.sbuf_tensor("l_o" + u, [128, 3 * TT], F32))
        gam_t = es3.enter_context(nc.sbuf_tensor("l_gam" + u, [128, NDC], F32))
        xnc2 = [[Buf(xn_t[:, (b_ * NDC + c) * TT:(b_ * NDC + c + 1) * TT]) for c in range(NDC)] for b_ in range(2)]
        xs = [Buf(xs_t[:, i * TT:(i + 1) * TT]) for i in range(3)]
        sq = [Buf(sq_t[:, i * TT:(i + 1) * TT]) for i in range(2)]
        rstd = Buf(rstd_t[:, :])
        wsl = [Buf(w_t[:, i * D:(i + 1) * D]) for i in range(3)]
        ob = [[Buf(o_t[:, i * TT + s * 512:i * TT + (s + 1) * 512]) for s in range(2)] for i in range(3)]
        gam = Buf(gam_t[:, :])
        d_xs = [k.new_dsem("lxs") for _ in range(3)]
        d_w = [k.new_dsem("lw", sw=True) for _ in range(3)]
        d_o = [k.new_dsem("lo") for _ in range(3)]
        d_xn = k.new_dsem("lxn")
        xs_i = _ring(3)
        sq_i = _ring(2)
        if mode == "norm":
            k.dma("sp", d_xn, gam.ap, gam_d, writes=[gam])
        wcnt = [0]
        pending = []

        def load_w(oc):
            s_ = wcnt[0] % 3
            wcnt[0] += 1
            k.dma("pool", d_w[s_], wsl[s_].ap, w[oc], writes=[wsl[s_]])
            pending.append(s_)

        seq = [(t, oc) for t in range(NT) for oc in range(n_out)]
        nxt = 0
        for _ in range(2):
            if nxt < len(seq):
                load_w(seq[nxt][1])
                nxt += 1
        grp = 0
        ring6 = _ring(6)
        if stat_chunks is not None:
            sqs_t = es3.enter_context(nc.sbuf_tensor("l_sqs" + u, [128, 2 * 512], BF16))
            stat_t = es3.enter_context(nc.sbuf_tensor("l_stat" + u, [128, len(stat_chunks) * 2 * NT + len(stat_chunks)], F32))
            sqs = [Buf(sqs_t[:, i * 512:(i + 1) * 512]) for i in range(2)]
            sqs_i = _ring(2)
            stat = Buf(stat_t[:, :])

        def load_xn(c, t_):
            i = next(xs_i)
            k.dma("sp", d_xs[i], xs[i].ap, src[c * 128:(c + 1) * 128, t_ * TT:(t_ + 1) * TT], writes=[xs[i]])
            return xs[i]

        nxt_units = []
        if mode == "norm":
            for un in make_norm_units(k, ps, src, 0, TT, load_xn, sq, sq_i, rstd, gam, xnc2[0], consts):
                un()
        for t in range(NT):
            xnc = xnc2[t % 2]
            if mode == "norm":
                while nxt_units:
                    nxt_units.pop(0)()
                if t + 1 < NT:
                    nxt_units = make_norm_units(k, ps, src, t + 1, TT, load_xn, sq, sq_i, rstd, gam, xnc2[(t + 1) % 2], consts)
            else:
                for c in range(NDC):
                    k.dma("sp", d_xn, xnc[c].ap, src[c * 128:(c + 1) * 128, t * TT:(t + 1) * TT], writes=[xnc[c]])
            for oc in range(n_out):
                if nxt_units and oc >= 2:
                    nxt_units.pop(0)()
                if nxt < len(seq):
                    load_w(seq[nxt][1])
                    nxt += 1
                ws = pending.pop(0)
                o = ob[grp % 3]
                if resid is not None:
                    i = next(xs_i)
                    k.dma("sp", d_xs[i], xs[i].ap, resid[oc * 128:(oc + 1) * 128, t * TT:(t + 1) * TT], writes=[xs[i]])
                    x = xs[i]
                for s in range(2):
                    p = ps[2 + next(ring6)]
                    for c in range(NDC):
                        k.op("pe", lambda e: e.matmul(p.ap, lhsT=wsl[ws].ap[:, c * 128:(c + 1) * 128],
                                                      rhs=xnc[c].ap[:, s * 512:(s + 1) * 512],
                                                      start=(c == 0), stop=(c == NDC - 1)),
                             reads=[wsl[ws], xnc[c]], writes=[p], sig=(c == NDC - 1))
                    if stat_chunks is not None and oc in stat_chunks:
                        j_ = stat_chunks.index(oc)
                        jq = sqs[next(sqs_i)]
                        k.op("act", lambda e: e.activation(out=jq.ap, in_=p.ap, func=AF.Square), reads=[p], writes=[jq])
                        p2 = ps[2 + next(ring6)]
                        k.op("pe", lambda e: e.matmul(p2.ap, lhsT=consts["ones"].ap, rhs=jq.ap, start=True, stop=True),
                             reads=[consts["ones"], jq], writes=[p2])
                        col = stat.ap[:, j_ * 2 * NT + 2 * t + s:j_ * 2 * NT + 2 * t + s + 1]
                        k.op("dve", lambda e: e.tensor_reduce(out=col, in_=p2.ap, axis=AX.X, op=ALU.max), reads=[p2], writes=[stat])
                    if resid is not None:
                        k.op("dve", lambda e: e.tensor_tensor(out=o[s].ap, in0=p.ap,
                                                              in1=x.ap[:, s * 512:(s + 1) * 512], op=ALU.add),
                             reads=[p, x], writes=[o[s]])
                    elif (2 * grp + s) % 2 == 0:
                        k.op("act", lambda e: e.copy(out=o[s].ap, in_=p.ap), reads=[p], writes=[o[s]])
                    else:
                        k.op("dve", lambda e: e.tensor_copy(out=o[s].ap, in_=p.ap), reads=[p], writes=[o[s]])
                k.dma("sp", d_o[grp % 3], dst[oc * 128:(oc + 1) * 128, t * TT:(t + 1) * TT],
                      o_t[:, (grp % 3) * TT:(grp % 3 + 1) * TT], reads=[o[0], o[1]])
                grp += 1
        if stat_chunks is not None:
            n_ = len(stat_chunks)
            fin = stat_t[:, n_ * 2 * NT:n_ * 2 * NT + n_]
            k.op("dve", lambda e: e.tensor_reduce(out=fin, in_=stat_t[:, 0:n_ * 2 * NT].rearrange("p (j r) -> p j r", r=2 * NT), axis=AX.X, op=ALU.max),
                 reads=[stat], writes=[stat])
            k.dma("sp", d_xn, stat_dst, fin, reads=[stat])
        k.end_phase()


def mlstm_phase(k, ps, Z, mixedT, prm, S, consts):
    nc = k.nc
    u = k.uid()
    NCH = S // 128
    NP = S // 1024
    ident = consts["ident"]
    one = consts["one"]
    epsb = consts["eps"]
    masks = {"f": consts["masku"], "b": consts["maskl"]}
    esel = consts["esel"]
    KS = 128 ** -0.5
    with ExitStack() as es4:
        w1T_t = es4.enter_context(nc.sbuf_tensor("m_w1T" + u, [128, 2 * NCH * 8], F32))
        w2T_t = es4.enter_context(nc.sbuf_tensor("m_w2T" + u, [128, 2 * NCH * 8], F32))
        thT_t = es4.enter_context(nc.sbuf_tensor("m_thT" + u, [128, 2 * NCH * 8], F32))
        wcb_t = es4.enter_context(nc.sbuf_tensor("m_wcb" + u, [128, 2 * NCH * 8], F32))
        cw_t = es4.enter_context(nc.sbuf_tensor("m_cw" + u, [128, 48], F32))
        mg_t = es4.enter_context(nc.sbuf_tensor("m_mg" + u, [128, 8], F32))
        W1T = {"f": Buf(w1T_t[:, 0:NCH * 8]), "b": Buf(w1T_t[:, NCH * 8:2 * NCH * 8])}
        W2T = {"f": Buf(w2T_t[:, 0:NCH * 8]), "b": Buf(w2T_t[:, NCH * 8:2 * NCH * 8])}
        THT = {"f": Buf(thT_t[:, 0:NCH * 8]), "b": Buf(thT_t[:, NCH * 8:2 * NCH * 8])}
        WCB = {"f": Buf(wcb_t[:, 0:NCH * 8]), "b": Buf(wcb_t[:, NCH * 8:2 * NCH * 8])}
        cw = Buf(cw_t[:, :])
        mg = Buf(mg_t[:, :])
        d_c = k.new_dsem("mc")
        k.dma("sp", d_c, cw.ap, prm["conv"], writes=[cw])
        k.dma("sp", d_c, mg.ap, prm["mg"], writes=[mg])
        with ExitStack() as es5:
            ig_t = es5.enter_context(nc.sbuf_tensor("g_ig" + u, [8, 2 * S], F32))
            fg_t = es5.enter_context(nc.sbuf_tensor("g_fg" + u, [8, 2 * S], F32))
            t1_t = es5.enter_context(nc.sbuf_tensor("g_t1" + u, [8, S], F32))
            t2_t = es5.enter_context(nc.sbuf_tensor("g_t2" + u, [8, S], F32))
            B_t = es5.enter_context(nc.sbuf_tensor("g_B" + u, [8, S], F32))
            A_t = es5.enter_context(nc.sbuf_tensor("g_A" + u, [8, S], F32))
            P_t = es5.enter_context(nc.sbuf_tensor("g_P" + u, [8, S], F32))
            P2_t = es5.enter_context(nc.sbuf_tensor("g_P2" + u, [8, S], F32))
            on_t = es5.enter_context(nc.sbuf_tensor("g_on" + u, [8, S], F32))
            bg_t = es5.enter_context(nc.sbuf_tensor("g_bg" + u, [8, 4], F32))
            sm_t = es5.enter_context(nc.sbuf_tensor("g_sm" + u, [8, 4 * NCH + 4], F32))
            igb_ = {"f": Buf(ig_t[:, 0:S]), "b": Buf(ig_t[:, S:2 * S])}
            fgb_ = {"f": Buf(fg_t[:, 0:S]), "b": Buf(fg_t[:, S:2 * S])}
            t1 = Buf(t1_t[:, :]); t2 = Buf(t2_t[:, :]); Bb = Buf(B_t[:, :]); Ab = Buf(A_t[:, :])
            Pb = Buf(P_t[:, :]); P2 = Buf(P2_t[:, :]); onesb = Buf(on_t[:, :]); bg = Buf(bg_t[:, :])
            pend = Buf(sm_t[:, 0:NCH]); pprev = Buf(sm_t[:, NCH:2 * NCH]); wc = Buf(sm_t[:, 2 * NCH:3 * NCH])
            tot = Buf(sm_t[:, 4 * NCH:4 * NCH + 1])
            d_g = k.new_dsem("mg")
            g0 = ZC_G * 128
            k.dma("sp", d_g, igb_["f"].ap, Z[g0 + 0:g0 + 8, :], writes=[igb_["f"]])
            k.dma("sp", d_g, fgb_["f"].ap, Z[g0 + 8:g0 + 16, :], writes=[fgb_["f"]])
            k.dma("sp", d_g, igb_["b"].ap, Z[g0 + 16:g0 + 24, :], writes=[igb_["b"]])
            k.dma("sp", d_g, fgb_["b"].ap, Z[g0 + 24:g0 + 32, :], writes=[fgb_["b"]])
            k.dma("sp", d_g, bg.ap, prm["bg"], writes=[bg])
            k.op("dve", lambda e: e.memset(onesb.ap, 1.0), writes=[onesb])
            for di, dr in enumerate(("f", "b")):
                ig = igb_[dr]; fg = fgb_[dr]
                k.op("dve", lambda e: e.tensor_scalar(out=t1.ap, in0=fg.ap, scalar1=bg.ap[:, 2 * di + 1:2 * di + 2], scalar2=None, op0=ALU.add),
                     reads=[fg, bg], writes=[t1])
                k.op("dve", lambda e: e.scalar_tensor_tensor(out=t2.ap, in0=t1.ap, scalar=-1.0, in1=t1.ap, op0=ALU.mult, op1=ALU.max), reads=[t1], writes=[t2])
                k.op("act", lambda e: e.activation(out=t2.ap, in_=t2.ap, func=AF.Exp, scale=-1.0), reads=[t2], writes=[t2])
                k.op("act", lambda e: e.activation(out=t2.ap, in_=t2.ap, func=AF.Ln, bias=one.ap[0:8, :], scale=1.0), reads=[t2, one], writes=[t2])
                k.op("dve", lambda e: e.scalar_tensor_tensor(out=t1.ap, in0=t1.ap, scalar=0.0, in1=t2.ap, op0=ALU.min, op1=ALU.subtract),
                     reads=[t1, t2], writes=[t1])
                k.op("dve", lambda e: e.tensor_scalar(out=ig.ap, in0=ig.ap, scalar1=bg.ap[:, 2 * di:2 * di + 1], scalar2=None, op0=ALU.add),
                     reads=[ig, bg], writes=[ig])
                k.op("dve", lambda e: e.tensor_tensor_scan(out=Bb.ap, data0=onesb.ap, data1=t1.ap, initial=0.0, op0=ALU.mult, op1=ALU.add),
                     reads=[onesb, t1], writes=[Bb])
                if dr == "b":
                    k.op("dve", lambda e: e.tensor_copy(out=tot.ap, in_=Bb.ap[:, S - 1:S]), reads=[Bb], writes=[tot])
                    k.op("dve", lambda e: e.tensor_tensor(out=Bb.ap, in0=t1.ap, in1=Bb.ap, op=ALU.subtract), reads=[t1, Bb], writes=[Bb])
                    k.op("dve", lambda e: e.tensor_scalar(out=Bb.ap, in0=Bb.ap, scalar1=tot.ap, scalar2=None, op0=ALU.add), reads=[Bb, tot], writes=[Bb])
                k.op("dve", lambda e: e.tensor_tensor(out=Ab.ap, in0=ig.ap, in1=Bb.ap, op=ALU.subtract), reads=[ig, Bb], writes=[Ab])
                if dr == "f":
                    k.op("dve", lambda e: e.tensor_tensor_scan(out=Pb.ap, data0=Ab.ap, data1=Ab.ap, initial=0.0, op0=ALU.max, op1=ALU.max),
                         reads=[Ab], writes=[Pb])
                    Pf = Pb
                else:
                    cur, nxt = Ab, Pb
                    first = True
                    sh = 1
                    while sh < S:
                        src_ = cur
                        k.op("dve", lambda e: e.tensor_tensor(out=nxt.ap[:, 0:S - sh], in0=src_.ap[:, 0:S - sh], in1=src_.ap[:, sh:S], op=ALU.max),
                             reads=[src_], writes=[nxt])
                        k.op("dve", lambda e: e.tensor_copy(out=nxt.ap[:, S - sh:S], in_=src_.ap[:, S - sh:S]), reads=[src_], writes=[nxt])
                        if first:
                            cur, nxt = Pb, P2
                            first = False
                        else:
                            cur, nxt = nxt, cur
                        sh *= 2
                    Pf = cur
                    k.op("dve", lambda e: e.tensor_scalar_max(out=Pf.ap, in0=Pf.ap, scalar1=0.0), reads=[Pf], writes=[Pf])
                Pv = Pf.ap.rearrange("p (c j) -> p c j", j=128)
                if dr == "f":
                    k.op("dve", lambda e: e.tensor_copy(out=pend.ap, in_=Pv[:, :, 127]), reads=[Pf], writes=[pend])
                    k.op("dve", lambda e: e.memset(pprev.ap[:, 0:1], 0.0), writes=[pprev])
                    if NCH > 1:
                        k.op("dve", lambda e: e.tensor_copy(out=pprev.ap[:, 1:NCH], in_=pend.ap[:, 0:NCH - 1]), reads=[pend], writes=[pprev])
                else:
                    k.op("dve", lambda e: e.tensor_copy(out=pend.ap, in_=Pv[:, :, 0]), reads=[Pf], writes=[pend])
                    k.op("dve", lambda e: e.memset(pprev.ap[:, NCH - 1:NCH], 0.0), writes=[pprev])
                    if NCH > 1:
                        k.op("dve", lambda e: e.tensor_copy(out=pprev.ap[:, 0:NCH - 1], in_=pend.ap[:, 1:NCH]), reads=[pend], writes=[pprev])
                Av = Ab.ap.rearrange("p (c j) -> p c j", j=128)
                Bv = Bb.ap.rearrange("p (c j) -> p c j", j=128)
                t2v = t2.ap.rearrange("p (c j) -> p c j", j=128)
                pprev_bc = pprev.ap.unsqueeze(2).to_broadcast([8, NCH, 128])
                pend_bc = pend.ap.unsqueeze(2).to_broadcast([8, NCH, 128])
                outs = []
                for which, dstT in (("w1", W1T[dr]), ("th", THT[dr])):
                    if which == "w1":
                        k.op("dve", lambda e: e.tensor_tensor(out=t2v, in0=Av, in1=pprev_bc, op=ALU.subtract), reads=[Ab, pprev], writes=[t2])
                        k.op("act", lambda e: e.activation(out=t2.ap, in_=t2.ap, func=AF.Exp), reads=[t2], writes=[t2])
                    elif which == "w2":
                        k.op("dve", lambda e: e.tensor_tensor(out=t2v, in0=Av, in1=pend_bc, op=ALU.subtract), reads=[Ab, pend], writes=[t2])
                        k.op("act", lambda e: e.activation(out=t2.ap, in_=t2.ap, func=AF.Exp), reads=[t2], writes=[t2])
                    else:
                        k.op("dve", lambda e: e.tensor_tensor(out=t2v, in0=Bv, in1=pprev_bc, op=ALU.add), reads=[Bb, pprev], writes=[t2])
                        k.op("act", lambda e: e.activation(out=t2.ap, in_=t2.ap, func=AF.Exp, scale=-1.0), reads=[t2], writes=[t2])
                    for c0 in range(0, NCH, 32):
                        pb = ps[0]
                        n = min(32, NCH - c0)
                        for cc in range(n):
                            ch = c0 + cc
                            k.op("pe", lambda e: e.transpose(out=pb.ap[:, cc * 8:(cc + 1) * 8], in_=t2.ap[:, ch * 128:(ch + 1) * 128],
                                                             identity=ident.ap[0:8, 0:8]),
                                 reads=[t2, ident], writes=[pb], sig=(cc == n - 1))
                        k.op("dve", lambda e: e.tensor_copy(out=dstT.ap[:, c0 * 8:(c0 + n) * 8], in_=pb.ap[:, 0:n * 8]), reads=[pb], writes=[dstT])
                k.op("dve", lambda e: e.tensor_tensor(out=wc.ap, in0=pprev.ap, in1=pend.ap, op=ALU.subtract), reads=[pprev, pend], writes=[wc])
                k.op("act", lambda e: e.activation(out=wc.ap, in_=wc.ap, func=AF.Exp), reads=[wc], writes=[wc])
                pb = ps[1]
                for h in range(8):
                    k.op("pe", lambda e: e.matmul(pb.ap[:, h * NCH:(h + 1) * NCH], lhsT=esel.ap[0:8, h * 128:(h + 1) * 128], rhs=wc.ap,
                                                  start=True, stop=True),
                         reads=[esel, wc], writes=[pb], sig=(h == 7))
                k.op("dve", lambda e: e.tensor_copy(out=WCB[dr].ap, in_=pb.ap[:, 0:8 * NCH]), reads=[pb], writes=[WCB[dr]])
            k.barrier()
        with ExitStack() as es6:
            HP = 2
            A_ = lambda nm, shp, dt: es6.enter_context(nc.sbuf_tensor(nm + u, shp, dt))
            slots = []
            for sl in range(HP):
                qT_t = A_(f"h_qT{sl}", [128, S], BF16); kT_t = A_(f"h_kT{sl}", [128, S], BF16); kt_t = A_(f"h_kt{sl}", [128, S], BF16)
                vt_t = A_(f"h_vt{sl}", [128, S], F32); hm_t = A_(f"h_hm{sl}", [128, S], F32)
                va_t = A_(f"h_va{sl}", [128, 2 * NCH * 130], BF16)
                c32_t = A_(f"h_C32{sl}", [128, 2 * 130], F32); cbf_t = A_(f"h_Cbf{sl}", [128, 2 * 130], BF16)
                slots.append({
                    "qT": Buf(qT_t[:, :]), "kT": Buf(kT_t[:, :]), "kt": Buf(kt_t[:, :]), "vt": Buf(vt_t[:, :]),
                    "hm": hm_t, "hmc": [Buf(hm_t[:, c_ * 128:(c_ + 1) * 128]) for c_ in range(NCH)],
                    "V1": {"f": Buf(va_t[:, 0:NCH * 130]), "b": Buf(va_t[:, NCH * 130:2 * NCH * 130])},
                    "T32": {"f": Buf(c32_t[:, 0:129]), "b": Buf(c32_t[:, 130:259])},
                    "Cbf": {"f": Buf(cbf_t[:, 0:129]), "b": Buf(cbf_t[:, 130:259])},
                })
            st_t = A_("h_st", [128, 3 * 1026], F32)
            cv_t = A_("h_cv", [128, 2 * 1024], F32)
            cv2_t = A_("h_cv2", [128, 2 * 1024], F32)
            sm_t = A_("h_sm", [128, 8 * 128], BF16)
            den_t = A_("h_den", [128, 16], F32)
            ss_t = A_("h_ss", [128, 2 * NCH], F32)
            ob_t = A_("h_ob", [128, 2 * 512], BF16)
            so_t = A_("h_so", [128, 2 * 512], F32)
            stg = [Buf(st_t[:, i * 1026:(i + 1) * 1026]) for i in range(3)]
            cv = [Buf(cv_t[:, i * 1024:(i + 1) * 1024]) for i in range(2)]
            cv2 = [Buf(cv2_t[:, i * 1024:(i + 1) * 1024]) for i in range(2)]
            cv2_i = _ring(2)
            smr = [Buf(sm_t[:, i * 128:(i + 1) * 128]) for i in range(8)]
            denr = [Buf(den_t[:, i:i + 1]) for i in range(16)]
            ss = Buf(ss_t[:, 0:NCH]); rs = Buf(ss_t[:, NCH:2 * NCH])
            obr = [Buf(ob_t[:, i * 512:(i + 1) * 512]) for i in range(2)]
            sor = [Buf(so_t[:, i * 512:(i + 1) * 512]) for i in range(2)]
            d_st = [k.new_dsem("mst") for _ in range(3)]
            d_ob = [k.new_dsem("mob") for _ in range(2)]
            d_so = [k.new_dsem("mso") for _ in range(2)]
            st_i = _ring(3); cv_i = _ring(2); sm_i = _ring(8); den_i = _ring(16); so_i = _ring(2); ob_i = _ring(2)
            pbank = _ring(8)

            def load_piece(zc, pc, halo):
                i = next(st_i)
                b = stg[i]
                t0 = pc * 1024
                if halo:
                    lo = t0 - 1 if pc > 0 else t0
                    hi = t0 + 1025 if pc < NP - 1 else t0 + 1024
                    if pc == 0:
                        k.op("pool", lambda e: e.memset(b.ap[:, 0:1], 0.0), writes=[b])
                    if pc == NP - 1:
                        k.op("pool", lambda e: e.memset(b.ap[:, 1025:1026], 0.0), writes=[b])
                    o0 = lo - (t0 - 1)
                    k.dma("sp", d_st[i], b.ap[:, o0:o0 + (hi - lo)], Z[zc * 128:(zc + 1) * 128, lo:hi], writes=[b])
                else:
                    k.dma("sp", d_st[i], b.ap[:, 0:1024], Z[zc * 128:(zc + 1) * 128, t0:t0 + 1024], writes=[b])
                return b

            def skew(pieces):
                units = []
                ns = len(pieces) + max(len(p) for p in pieces)
                for slot in range(ns):
                    for p in range(len(pieces)):
                        s_ = slot - p
                        if 0 <= s_ < len(pieces[p]) and pieces[p][s_] is not None:
                            units.append(pieces[p][s_])
                return units

            def prep(h, sl):
                qT = sl["qT"]; kT = sl["kT"]; kt = sl["kt"]; vt = sl["vt"]
                pieces = []
                for pc in range(NP):
                    for which in ("q", "k"):
                        zc = (ZC_Q if which == "q" else ZC_K) + h
                        wofs = zc * 3
                        box = {}

                        def s0(zc=zc, pc=pc, box=box):
                            box["b"] = load_piece(zc, pc, True)

                        def s1(box=box, wofs=wofs):
                            b = box["b"]
                            c = cv[next(cv_i)]
                            box["c"] = c
                            k.op("act", lambda e: e.mul(out=c.ap, in_=b.ap[:, 1:1025], mul=cw.ap[:, wofs + 1:wofs + 2]), reads=[b, cw], writes=[c])

                        def s2(box=box, wofs=wofs):
                            b = box["b"]; c = box["c"]
                            k.op("dve", lambda e: e.scalar_tensor_tensor(out=c.ap, in0=b.ap[:, 0:1024], scalar=cw.ap[:, wofs:wofs + 1], in1=c.ap, op0=ALU.mult, op1=ALU.add),
                                 reads=[b, cw, c], writes=[c])
                            k.op("dve", lambda e: e.scalar_tensor_tensor(out=c.ap, in0=b.ap[:, 2:1026], scalar=cw.ap[:, wofs + 2:wofs + 3], in1=c.ap, op0=ALU.mult, op1=ALU.add),
                                 reads=[b, cw, c], writes=[c])

                        if which == "q":
                            def s3(box=box, pc=pc):
                                c = box["c"]
                                k.op("act", lambda e: e.activation(out=qT.ap[:, pc * 1024:(pc + 1) * 1024], in_=c.ap, func=AF.Silu), reads=[c], writes=[qT])
                            pieces.append([s0, s1, s2, s3])
                        else:
                            def s3(box=box, pc=pc):
                                c = box["c"]
                                k.op("act", lambda e: e.activation(out=c.ap, in_=c.ap, func=AF.Silu), reads=[c], writes=[c])
                                k.op("act", lambda e: e.mul(out=kT.ap[:, pc * 1024:(pc + 1) * 1024], in_=c.ap, mul=KS), reads=[c], writes=[kT])

                            def s4(pc=pc):
                                pb = ps[next(pbank)]
                                pbv = pb.ap.bitcast(BF16)
                                for j in range(8):
                                    ch = pc * 8 + j
                                    k.op("pe", lambda e: e.transpose(out=pbv[:, j * 128:(j + 1) * 128], in_=kT.ap[:, ch * 128:(ch + 1) * 128],
                                                                     identity=consts["identb"].ap),
                                         reads=[kT, consts["identb"]], writes=[pb], sig=(j == 7))
                                k.op("act", lambda e: e.copy(out=kt.ap[:, pc * 1024:(pc + 1) * 1024], in_=pbv[:, 0:1024]), reads=[pb], writes=[kt])
                            pieces.append([s0, s1, s2, s3, s4])
                    box = {}

                    def v0(pc=pc, box=box):
                        box["b"] = load_piece(ZC_V + h, pc, False)

                    def v1(half, pc=pc, box=box):
                        b = box["b"]
                        pb = ps[next(pbank)]
                        for j in range(4):
                            cj = half * 4 + j
                            k.op("pe", lambda e: e.transpose(out=pb.ap[:, j * 128:(j + 1) * 128], in_=b.ap[:, cj * 128:(cj + 1) * 128], identity=ident.ap),
                                 reads=[b, ident], writes=[pb], sig=(j == 3))
                        k.op("act", lambda e: e.copy(out=vt.ap[:, pc * 1024 + half * 512:pc * 1024 + (half + 1) * 512], in_=pb.ap), reads=[pb], writes=[vt])
                    pieces.append([v0, (lambda v1=v1: v1(0)), (lambda v1=v1: v1(1))])
                for un in skew(pieces):
                    un()
                vt3_ = vt.ap.rearrange("p (c j) -> p c j", j=128)
                for dr in ("f", "b"):
                    WT = W1T[dr]; VV = sl["V1"][dr]
                    wv = WT.ap.rearrange("p (c g) -> p c g", g=8)[:, :, h:h + 1]
                    vv = VV.ap.rearrange("p (c j) -> p c j", j=130)
                    k.op("dve", lambda e: e.tensor_tensor(out=vv[:, :, 0:128], in0=vt3_, in1=wv.to_broadcast([128, NCH, 128]), op=ALU.mult),
                         reads=[vt, WT], writes=[VV])
                    k.op("dve", lambda e: e.tensor_copy(out=vv[:, :, 128:129], in_=wv), reads=[WT], writes=[VV])

            def scan(chains):
                def geo(h, dr, i):
                    ch = i if dr == "f" else NCH - 1 - i
                    chp = ch - 1 if dr == "f" else ch + 1
                    return ch, chp, slice(ch * 128, (ch + 1) * 128)
                smb = {}

                def emit_sm(i):
                    pas = []
                    for (h, sl, dr) in chains:
                        ch, chp, cs = geo(h, dr, i)
                        pA = ps[next(pbank)]
                        k.op("pe", lambda e: e.matmul(pA.ap[:, 0:128], lhsT=sl["kT"].ap[:, cs], rhs=sl["qT"].ap[:, cs], start=True, stop=True),
                             reads=[sl["kT"], sl["qT"]], writes=[pA])
                        pas.append(pA)
                    for n_, (h, sl, dr) in enumerate(chains):
                        sm = smr[next(sm_i)]
                        k.op("dve", lambda e: e.tensor_tensor(out=sm.ap, in0=pas[n_].ap[:, 0:128], in1=masks[dr].ap, op=ALU.mult),
                             reads=[pas[n_], masks[dr]], writes=[sm])
                        smb[(i, n_)] = sm
                emit_sm(0)
                for i in range(NCH):
                    pCs = []
                    if i < NCH - 1:
                        for (h, sl, dr) in chains:
                            ch, chp, cs = geo(h, dr, i)
                            pC = ps[next(pbank)]
                            v1ap = sl["V1"][dr].ap[:, ch * 130:ch * 130 + 129]
                            k.op("pe", lambda e: e.matmul(pC.ap[:, 0:129], lhsT=sl["kt"].ap[:, cs], rhs=v1ap, start=True, stop=True),
                                 reads=[sl["kt"], sl["V1"][dr]], writes=[pC])
                            pCs.append(pC)
                    pBs = []
                    for n_, (h, sl, dr) in enumerate(chains):
                        ch, chp, cs = geo(h, dr, i)
                        pB = ps[next(pbank)]
                        v1ap = sl["V1"][dr].ap[:, ch * 130:ch * 130 + 129]
                        sm = smb.pop((i, n_))
                        k.op("pe", lambda e: e.matmul(pB.ap[:, 0:129], lhsT=sm.ap, rhs=v1ap, start=True, stop=(i == 0)),
                             reads=[sm, sl["V1"][dr]], writes=[pB], sig=(i == 0))
                        if i > 0:
                            k.op("pe", lambda e: e.matmul(pB.ap[:, 0:129], lhsT=sl["qT"].ap[:, cs], rhs=sl["Cbf"][dr].ap, start=False, stop=True),
                                 reads=[sl["qT"], sl["Cbf"][dr]], writes=[pB])
                        pBs.append(pB)
                    if i < NCH - 1:
                        for n_, (h, sl, dr) in enumerate(chains):
                            ch, chp, cs = geo(h, dr, i)
                            T32 = sl["T32"][dr]
                            widxp = h * NCH + chp
                            if i == 0:
                                k.op("dve", lambda e: e.tensor_copy(out=T32.ap, in_=pCs[n_].ap[:, 0:129]), reads=[pCs[n_]], writes=[T32])
                            else:
                                k.op("dve", lambda e: e.scalar_tensor_tensor(out=T32.ap, in0=T32.ap, scalar=WCB[dr].ap[:, widxp:widxp + 1],
                                                                             in1=pCs[n_].ap[:, 0:129], op0=ALU.mult, op1=ALU.add),
                                     reads=[T32, WCB[dr], pCs[n_]], writes=[T32])
                        for n_, (h, sl, dr) in enumerate(chains):
                            ch, chp, cs = geo(h, dr, i)
                            widx = h * NCH + ch
                            k.op("act", lambda e: e.mul(out=sl["Cbf"][dr].ap, in_=sl["T32"][dr].ap, mul=WCB[dr].ap[:, widx:widx + 1]),
                                 reads=[sl["T32"][dr], WCB[dr]], writes=[sl["Cbf"][dr]])
                        emit_sm(i + 1)
                    dens = [denr[next(den_i)] for _ in chains]
                    for n_, (h, sl, dr) in enumerate(chains):
                        k.op("act", lambda e: e.activation(out=dens[n_].ap, in_=pBs[n_].ap[:, 128:129], func=AF.Abs), reads=[pBs[n_]], writes=[dens[n_]])
                    for n_, (h, sl, dr) in enumerate(chains):
                        ch, chp, cs = geo(h, dr, i)
                        idx = ch * 8 + h
                        k.op("dve", lambda e: e.tensor_scalar(out=dens[n_].ap, in0=dens[n_].ap, scalar1=THT[dr].ap[:, idx:idx + 1], scalar2=None, op0=ALU.max),
                             reads=[dens[n_], THT[dr]], writes=[dens[n_]])
                    for n_, (h, sl, dr) in enumerate(chains):
                        k.op("dve", lambda e: e.reciprocal(out=dens[n_].ap, in_=dens[n_].ap), reads=[dens[n_]], writes=[dens[n_]])
                    for n_, (h, sl, dr) in enumerate(chains):
                        ch, chp, cs = geo(h, dr, i)
                        hc = sl["hmc"][ch]
                        if i < NCH // 2:
                            k.op("act", lambda e: e.mul(out=hc.ap, in_=pBs[n_].ap[:, 0:128], mul=dens[n_].ap), reads=[pBs[n_], dens[n_]], writes=[hc])
                        else:
                            k.op("dve", lambda e: e.scalar_tensor_tensor(out=hc.ap, in0=pBs[n_].ap[:, 0:128], scalar=dens[n_].ap, in1=hc.ap, op0=ALU.mult, op1=ALU.add),
                                 reads=[pBs[n_], dens[n_], hc], writes=[hc])

            def post(h, sl):
                hm_t = sl["hm"]; hmc = sl["hmc"]; vt = sl["vt"]
                hm3 = hm_t[:, :].rearrange("p (c j) -> p c j", j=128)
                k.op("act", lambda e: e.activation(out=vt.ap, in_=hm_t[:, :], func=AF.Square), reads=hmc, writes=[vt])
                k.op("dve", lambda e: e.tensor_reduce(out=ss.ap, in_=vt.ap.rearrange("p (c j) -> p c j", j=128), axis=AX.X, op=ALU.add),
                     reads=[vt], writes=[ss])
                k.op("act", lambda e: e.activation(out=rs.ap, in_=ss.ap, func=AF.Sqrt, bias=epsb.ap, scale=1.0 / 128), reads=[ss, epsb], writes=[rs])
                k.op("dve", lambda e: e.reciprocal(out=rs.ap, in_=rs.ap), reads=[rs], writes=[rs])
                k.op("dve", lambda e: e.tensor_tensor(out=hm3, in0=hm3, in1=rs.ap.unsqueeze(2).to_broadcast([128, NCH, 128]), op=ALU.mult),
                     reads=hmc + [rs], writes=hmc)
                for g4 in range(NCH // 4):
                    si = next(so_i)
                    sb_ = sor[si]
                    k.dma("sp", d_so[si], sb_.ap, Z[(ZC_O + h) * 128:(ZC_O + h + 1) * 128, g4 * 512:(g4 + 1) * 512], writes=[sb_])
                    k.op("act", lambda e: e.activation(out=sb_.ap, in_=sb_.ap, func=AF.Sigmoid), reads=[sb_], writes=[sb_])
                    pb = ps[next(pbank)]
                    for j in range(4):
                        ch = g4 * 4 + j
                        k.op("pe", lambda e: e.transpose(out=pb.ap[:, j * 128:(j + 1) * 128], in_=hmc[ch].ap, identity=ident.ap),
                             reads=[hmc[ch], ident], writes=[pb], sig=(j == 3))
                    oi = next(ob_i)
                    o = obr[oi]
                    k.op("dve", lambda e: e.scalar_tensor_tensor(out=o.ap, in0=pb.ap, scalar=mg.ap[:, h:h + 1], in1=sb_.ap,
                                                                 op0=ALU.mult, op1=ALU.mult),
                         reads=[pb, mg, sb_], writes=[o])
                    k.dma("sp", d_ob[oi], mixedT[h * 128:(h + 1) * 128, g4 * 512:(g4 + 1) * 512], o.ap, reads=[o])

            for h0 in range(0, MH, HP):
                for j in range(HP):
                    prep(h0 + j, slots[j])
                scan([(h0 + j, slots[j], dr) for j in range(HP) for dr in ("f", "b")])
                for j in range(HP):
                    post(h0 + j, slots[j])
            k.end_phase()


def attn_phase(k, ps, Z, mixedT, prm, S, consts, lambda_init):
    nc = k.nc
    u = k.uid()
    NKB = S // 128
    NQB = S // 512
    NP = S // 1024
    SC = 128 ** -0.5
    VW = 258
    ident = consts["ident"]
    ones = consts["ones"]
    epsb = consts["eps"]
    with ExitStack() as es:
        A_ = lambda nm, shp, dt: es.enter_context(nc.sbuf_tensor(nm + u, shp, dt))
        cos_t = A_("a_cos", [128, S], F32); sin_t = A_("a_sin", [128, S], F32)
        sets = []
        for par in range(2):
            qT_t = A_(f"a_qT{par}", [128, 2 * S], BF16); kT_t = A_(f"a_kT{par}", [128, 2 * S], BF16)
            vt_t = A_(f"a_vt{par}", [128, NKB * VW], BF16)
            sets.append({"qT": Buf(qT_t[:, :]), "kT": Buf(kT_t[:, :]), "vt": Buf(vt_t[:, :])})
        x_t = A_("a_x", [128, 3 * 1024], F32); xw_t = A_("a_xw", [128, 3 * 1024], F32)
        E_t = A_("a_E", [128, 4 * 512], BF16)
        oc_t = A_("a_oc", [128, 2 * 4 * 256], F32); oa_t = A_("a_oa", [128, 4 * 256], F32)
        jk_t = A_("a_jk", [128, 2 * 512], BF16)
        lv_t = A_("a_lv", [128, 4 * 128], F32)
        sm_t = A_("a_sm", [128, 80], F32)
        sg_t = A_("a_sg", [128, 2], F32)
        ob_t = A_("a_ob", [128, 2 * 512], BF16)
        cosb = Buf(cos_t[:, :]); sinb = Buf(sin_t[:, :])
        xr = [Buf(x_t[:, i * 1024:(i + 1) * 1024]) for i in range(3)]
        xwr = [Buf(xw_t[:, i * 1024:(i + 1) * 1024]) for i in range(3)]
        Er = [Buf(E_t[:, i * 512:(i + 1) * 512]) for i in range(4)]
        ocb = [[Buf(oc_t[:, (c * 4 + q) * 256:(c * 4 + q + 1) * 256]) for q in range(4)] for c in range(2)]
        oab = [Buf(oa_t[:, q * 256:(q + 1) * 256]) for q in range(4)]
        jkr = [Buf(jk_t[:, i * 512:(i + 1) * 512]) for i in range(2)]
        lv = Buf(lv_t[:, :])
        smalls = [Buf(sm_t[:, i:i + 1]) for i in range(80)]
        sg = Buf(sg_t[:, :])
        obr = [Buf(ob_t[:, i * 512:(i + 1) * 512]) for i in range(2)]
        d_c = k.new_dsem("ac")
        d_x = [k.new_dsem("ax") for _ in range(3)]
        d_ob = [k.new_dsem("aob") for _ in range(2)]
        k.dma("sp", d_c, cosb.ap, prm["cos"], writes=[cosb])
        k.dma("sp", d_c, sinb.ap, prm["sin"], writes=[sinb])
        k.dma("sp", d_c, lv.ap, prm["lamv"], writes=[lv])
        k.dma("sp", d_c, sg.ap, prm["sg"], writes=[sg])
        qkb_t = A_("a_qkb", [128, 16], F32)
        qkb = Buf(qkb_t[:, :])
        k.dma("sp", d_c, qkb.ap, prm["qkb"], writes=[qkb])
        e1, e2, lamn = smalls[0], smalls[1], smalls[2]
        hs = [{"q2m": smalls[4 + 8 * p], "k2m": smalls[5 + 8 * p], "tq": smalls[6 + 8 * p], "tk": smalls[7 + 8 * p],
               "negc": [smalls[8 + 8 * p], smalls[9 + 8 * p]]} for p in range(2)]
        x_i = _ring(3)
        jk_i = _ring(2)
        for j, dst_ in ((0, e1), (1, e2)):
            b = xr[j]
            k.op("dve", lambda e: e.tensor_tensor(out=b.ap[:, 0:128], in0=lv.ap[:, (2 * j) * 128:(2 * j + 1) * 128],
                                                  in1=lv.ap[:, (2 * j + 1) * 128:(2 * j + 2) * 128], op=ALU.mult), reads=[lv], writes=[b])
            k.op("dve", lambda e: e.tensor_reduce(out=dst_.ap, in_=b.ap[:, 0:128], axis=AX.X, op=ALU.add), reads=[b], writes=[dst_])
            k.op("act", lambda e: e.activation(out=dst_.ap, in_=dst_.ap, func=AF.Exp), reads=[dst_], writes=[dst_])
        k.op("dve", lambda e: e.tensor_tensor(out=lamn.ap, in0=e2.ap, in1=e1.ap, op=ALU.subtract), reads=[e1, e2], writes=[lamn])
        k.op("dve", lambda e: e.tensor_scalar(out=lamn.ap, in0=lamn.ap, scalar1=-float(lambda_init), scalar2=None, op0=ALU.add), reads=[lamn], writes=[lamn])
        for par in range(2):
            vt3 = sets[par]["vt"].ap.rearrange("p (k e) -> p k e", e=VW)
            k.op("dve", lambda e: e.memset(vt3[:, :, 256:258], 1.0), writes=[sets[par]["vt"]])
        E_i = _ring(4)
        stbank = _ring(3)
        sm_i = [24]

        def new_small():
            sm_i[0] = 24 + (sm_i[0] - 24 + 1) % 56
            return smalls[sm_i[0]]

        acc = [ps[4 + q] for q in range(4)]
        pmisc = ps[3]

        def prep_units(h):
            par = h % 2
            st_ = sets[par]; hp = hs[par]
            pieces = []
            for c in range(2):
                for which, dstb, zbase, mx, tmpm in (("q", st_["qT"], ZC_QA, hp["q2m"], hp["tq"]), ("k", st_["kT"], ZC_KA, hp["k2m"], hp["tk"])):
                    zc = zbase + 2 * h + c
                    for pc in range(NP):
                        box = {}
                        tsl = slice(pc * 1024, (pc + 1) * 1024)

                        def sA(zc=zc, tsl=tsl, box=box):
                            i = next(x_i)
                            box["x"] = xr[i]; box["xw"] = xwr[i]
                            x = xr[i]; xw = xwr[i]
                            k.dma("sp", d_x[i], x.ap, Z[zc * 128:(zc + 1) * 128, tsl], writes=[x])
                            k.dma("sp", d_x[i], xw.ap[0:64, :], Z[zc * 128 + 64:(zc + 1) * 128, tsl], writes=[xw])
                            k.dma("sp", d_x[i], xw.ap[64:128, :], Z[zc * 128:zc * 128 + 64, tsl], writes=[xw])

                        def sB(tsl=tsl, box=box, dstb=dstb, c=c, pc=pc):
                            x = box["x"]; xw = box["xw"]
                            k.op("dve", lambda e: e.tensor_tensor(out=x.ap, in0=x.ap, in1=cosb.ap[:, tsl], op=ALU.mult), reads=[x, cosb], writes=[x])
                            k.op("dve", lambda e: e.tensor_tensor(out=xw.ap, in0=xw.ap, in1=sinb.ap[:, tsl], op=ALU.mult), reads=[xw, sinb], writes=[xw])
                            k.op("dve", lambda e: e.tensor_tensor(out=dstb.ap[:, c * S + pc * 1024:c * S + (pc + 1) * 1024], in0=x.ap, in1=xw.ap, op=ALU.add),
                                 reads=[x, xw], writes=[dstb])

                        pieces.append([sA, sB])

                def sN(c=c):
                    ng = hp["negc"][c]
                    jq = 2 * h + c
                    jk_ = 8 + 2 * h + c
                    k.op("dve", lambda e: e.tensor_tensor(out=ng.ap, in0=qkb.ap[:, jq:jq + 1], in1=qkb.ap[:, jk_:jk_ + 1], op=ALU.mult), reads=[qkb], writes=[ng])
                    k.op("act", lambda e: e.activation(out=ng.ap, in_=ng.ap, func=AF.Sqrt, scale=SC * SC * 1.04), reads=[ng], writes=[ng])
                    k.op("dve", lambda e: e.tensor_scalar(out=ng.ap, in0=ng.ap, scalar1=-1.0, scalar2=None, op0=ALU.mult), reads=[ng], writes=[ng])
                pieces.append([sN])
            vt3 = st_["vt"].ap.rearrange("p (k e) -> p k e", e=VW)
            for ec in range(2):
                zc = ZC_VA + 2 * h + ec
                for pc in range(NP):
                    box = {}

                    def vA(zc=zc, pc=pc, box=box):
                        i = next(x_i)
                        box["x"] = xr[i]
                        k.dma("sp", d_x[i], xr[i].ap, Z[zc * 128:(zc + 1) * 128, pc * 1024:(pc + 1) * 1024], writes=[xr[i]])

                    def vB(half, ec=ec, pc=pc, box=box):
                        x = box["x"]
                        for j in range(4):
                            cj = half * 4 + j
                            k.op("pe", lambda e: e.transpose(out=pmisc.ap[:, j * 128:(j + 1) * 128], in_=x.ap[:, cj * 128:(cj + 1) * 128], identity=ident.ap),
                                 reads=[x, ident], writes=[pmisc], sig=(j == 3))
                        kb0 = pc * 8 + half * 4
                        k.op("dve", lambda e: e.tensor_copy(out=vt3[:, kb0:kb0 + 4, ec * 128:(ec + 1) * 128], in_=pmisc.ap.rearrange("p (k e) -> p k e", e=128)),
                             reads=[pmisc], writes=[st_["vt"]])
                    pieces.append([vA, None, (lambda vB=vB: vB(0)), (lambda vB=vB: vB(1))])
            units = []
            nslot = len(pieces) + 6
            for slot in range(nslot):
                for p in range(len(pieces)):
                    s_ = slot - p
                    if 0 <= s_ < len(pieces[p]) and pieces[p][s_] is not None:
                        units.append(pieces[p][s_])
            return units

        pending_units = prep_units(0)
        while pending_units:
            pending_units.pop(0)()
        for h in range(AH):
            par = h % 2
            qT = sets[par]["qT"]; kT = sets[par]["kT"]; vt = sets[par]["vt"]
            negc = hs[par]["negc"]
            pending_units = prep_units(h + 1) if h + 1 < AH else []
            iters = [(qb, c, kb) for qb in range(NQB) for c in range(2) for kb in range(NKB)]
            LA = 2
            DEFER = 6
            PUMP = max(1, (len(iters) - 8) // max(1, len(pending_units) + 1))
            pS_of = {}

            def emit_st(i):
                qb, c, kb = iters[i]
                pS = ps[next(stbank)]
                k.op("pe", lambda e: e.matmul(pS.ap, lhsT=kT.ap[:, c * S + kb * 128:c * S + (kb + 1) * 128],
                                              rhs=qT.ap[:, c * S + qb * 512:c * S + (qb + 1) * 512], start=True, stop=True),
                     reads=[kT, qT], writes=[pS])
                pS_of[i] = pS

            def emit_out(qb):
                for ec in range(2):
                    for q in range(4):
                        k.op("pe", lambda e: e.transpose(out=pmisc.ap[:, q * 128:(q + 1) * 128], in_=oab[q].ap[:, ec * 128:(ec + 1) * 128], identity=ident.ap),
                             reads=[oab[q], ident], writes=[pmisc], sig=(q == 3))
                    o = obr[ec]
                    k.op("dve", lambda e: e.tensor_scalar(out=o.ap, in0=pmisc.ap, scalar1=sg.ap[:, ec:ec + 1], scalar2=float(1.0 - lambda_init), op0=ALU.mult, op1=ALU.mult),
                         reads=[pmisc, sg], writes=[o])
                    r0 = 1024 + h * 256 + ec * 128
                    k.dma("sp", d_ob[ec], mixedT[r0:r0 + 128, qb * 512:(qb + 1) * 512], o.ap, reads=[o])

            deferred = []
            for i in range(min(LA, len(iters))):
                emit_st(i)
            for i, (qb, c, kb) in enumerate(iters):
                if i + LA < len(iters):
                    emit_st(i + LA)
                pS = pS_of.pop(i)
                E = Er[next(E_i)]
                k.op("act", lambda e: e.activation(out=E.ap, in_=pS.ap, func=AF.Exp, bias=negc[c].ap, scale=SC),
                     reads=[pS, negc[c]], writes=[E])
                for q in range(4):
                    k.op("pe", lambda e: e.matmul(acc[q].ap[:, 0:257], lhsT=E.ap[:, q * 128:(q + 1) * 128], rhs=vt.ap[:, kb * VW:kb * VW + 257],
                                                  start=(kb == 0), stop=(kb == NKB - 1)),
                         reads=[E, vt], writes=[acc[q]], sig=(kb == NKB - 1 or q == 3))
                if deferred and deferred[0][0] <= i:
                    deferred.pop(0)[1]()
                elif pending_units and i % PUMP == PUMP - 1:
                    pending_units.pop(0)()
                if kb == NKB - 1:
                    for q in range(4):
                        rd = new_small()
                        k.op("dve", lambda e: e.reciprocal(out=rd.ap, in_=acc[q].ap[:, 256:257]), reads=[acc[q]], writes=[rd])
                        k.op("dve", lambda e: e.tensor_scalar(out=ocb[c][q].ap, in0=acc[q].ap[:, 0:256], scalar1=rd.ap, scalar2=None, op0=ALU.mult),
                             reads=[acc[q], rd], writes=[ocb[c][q]])
                    if c == 1:
                        for q in range(4):
                            oa = oab[q]
                            k.op("dve", lambda e: e.scalar_tensor_tensor(out=oa.ap, in0=ocb[1][q].ap, scalar=lamn.ap, in1=ocb[0][q].ap, op0=ALU.mult, op1=ALU.add),
                                 reads=[ocb[1][q], ocb[0][q], lamn], writes=[oa])
                            ssq = new_small()
                            k.op("act", lambda e: e.activation(out=ocb[1][q].ap, in_=oa.ap, func=AF.Square, accum_out=ssq.ap), reads=[oa], writes=[ocb[1][q], ssq])
                            k.op("act", lambda e: e.activation(out=ssq.ap, in_=ssq.ap, func=AF.Sqrt, bias=epsb.ap, scale=1.0 / 256), reads=[ssq, epsb], writes=[ssq])
                            k.op("dve", lambda e: e.reciprocal(out=ssq.ap, in_=ssq.ap), reads=[ssq], writes=[ssq])
                            k.op("dve", lambda e: e.tensor_scalar(out=oa.ap, in0=oa.ap, scalar1=ssq.ap, scalar2=None, op0=ALU.mult), reads=[oa, ssq], writes=[oa])
                        deferred.append((i + DEFER, (lambda qb_=qb: emit_out(qb_))))
            while deferred:
                deferred.pop(0)[1]()
            while pending_units:
                pending_units.pop(0)()
        k.end_phase()


def build_program(S=SEQ, depth=DEPTH, phases=("ffn1", "mix", "ffn2"), debug=False):
    import math
    nc = bass.Bass("TRN2", target_bir_lowering=False)
    xT = nc.dram_tensor("xT", [D, S], F32, kind="ExternalInput").ap()
    yT = nc.dram_tensor("yT", [D, S], F32, kind="ExternalOutput").ap()
    gam_all = nc.dram_tensor("gam_all", [3 * depth + 1, 128, NDC], F32, kind="ExternalInput").ap()
    cst = nc.dram_tensor("cst", [128, 3 * 128 + 1024], F32, kind="ExternalInput").ap()
    ropec = nc.dram_tensor("ropec", [128, S], F32, kind="ExternalInput").ap()
    ropes = nc.dram_tensor("ropes", [128, S], F32, kind="ExternalInput").ap()
    wts = {}
    prms = {}
    for l in range(depth):
        for nm in ("ffn1", "ffn2"):
            wts[(l, nm, "g")] = nc.dram_tensor(f"{nm}_wg{l}", [NFC, 128, D], F32, kind="ExternalInput").ap()
            wts[(l, nm, "u")] = nc.dram_tensor(f"{nm}_wu{l}", [NFC, 128, D], F32, kind="ExternalInput").ap()
            wts[(l, nm, "d")] = nc.dram_tensor(f"{nm}_wd{l}", [NDC, 128, NFC * 128], F32, kind="ExternalInput").ap()
        wts[(l, "win")] = nc.dram_tensor(f"win{l}", [NZC, 128, D], F32, kind="ExternalInput").ap()
        wts[(l, "wout")] = nc.dram_tensor(f"wout{l}", [NDC, 128, D], F32, kind="ExternalInput").ap()
        prms[l] = {
            "conv": nc.dram_tensor(f"convw{l}", [128, 48], F32, kind="ExternalInput").ap(),
            "bg": nc.dram_tensor(f"bgate{l}", [8, 4], F32, kind="ExternalInput").ap(),
            "mg": nc.dram_tensor(f"mnorm{l}", [128, 8], F32, kind="ExternalInput").ap(),
            "lamv": nc.dram_tensor(f"lamv{l}", [128, 512], F32, kind="ExternalInput").ap(),
            "sg": nc.dram_tensor(f"subln{l}", [128, 2], F32, kind="ExternalInput").ap(),
            "cos": ropec, "sin": ropes,
        }
    skind = "ExternalOutput" if debug else "Internal"
    X = nc.dram_tensor("Xres", [D, S], F32, kind=skind).ap()
    Z = nc.dram_tensor("Zproj", [NZC * 128, S], F32, kind=skind).ap()
    MT = nc.dram_tensor("mixedT", [D, S], BF16, kind=skind).ap()
    QKB = nc.dram_tensor("qkbound", [128, 16], F32, kind=skind).ap()
    k = K(nc)
    with ExitStack() as es8:
        ones_t = es8.enter_context(nc.sbuf_tensor("c_ones", [128, 128], BF16))
        eps_t = es8.enter_context(nc.sbuf_tensor("c_eps", [128, 1], F32))
        one_t = es8.enter_context(nc.sbuf_tensor("c_one", [128, 1], F32))
        cst_t = es8.enter_context(nc.sbuf_tensor("c_cst", [128, 3 * 128 + 1024], F32))
        idb_t = es8.enter_context(nc.sbuf_tensor("c_idb", [128, 128], BF16))
        pst = [nc.alloc_psum_tensor(f"psb{i}", [128, 512], F32) for i in range(8)]
        ps = [Buf(p[:, :]) for p in pst]
        ones = Buf(ones_t[:, :])
        k.op("dve", lambda e: e.memset(ones.ap, 1.0), writes=[ones])
        epsb = Buf(eps_t[:, :])
        k.op("dve", lambda e: e.memset(epsb.ap, EPS), writes=[epsb])
        oneb = Buf(one_t[:, :])
        k.op("dve", lambda e: e.memset(oneb.ap, 1.0), writes=[oneb])
        cstb = Buf(cst_t[:, :])
        d_cst = k.new_dsem("cst")
        k.dma("sp", d_cst, cstb.ap, cst, writes=[cstb])
        ident = Buf(cst_t[:, 0:128]); masku = Buf(cst_t[:, 128:256]); maskl = Buf(cst_t[:, 256:384]); esel = Buf(cst_t[:, 384:384 + 1024])
        for b in (ident, masku, maskl, esel):
            b.w = cstb.w
        identb = Buf(idb_t[:, :])
        k.op("dve", lambda e: e.tensor_copy(out=identb.ap, in_=ident.ap), reads=[ident], writes=[identb])
        consts = {"ones": ones, "eps": epsb, "one": oneb, "ident": ident, "identb": identb, "masku": masku, "maskl": maskl, "esel": esel}
        k.barrier()
        cur = xT
        for l in range(depth):
            lambda_init = 0.8 - 0.6 * math.exp(-0.3 * l)
            if "ffn1" in phases:
                ffn_phase(k, ps, cur, X, gam_all[3 * l + 0], wts[(l, "ffn1", "g")], wts[(l, "ffn1", "u")],
                          wts[(l, "ffn1", "d")], S, consts)
                cur = X
            if "mix" in phases:
                prms[l]["qkb"] = QKB
                linear_phase(k, ps, "norm", cur, gam_all[3 * l + 1], wts[(l, "win")], NZC, Z, S, consts,
                             stat_chunks=list(range(ZC_QA, ZC_QA + 8)) + list(range(ZC_KA, ZC_KA + 8)), stat_dst=QKB)
                mlstm_phase(k, ps, Z, MT, prms[l], S, consts)
                attn_phase(k, ps, Z, MT, prms[l], S, consts, lambda_init)
                linear_phase(k, ps, "bf16", MT, None, wts[(l, "wout")], NDC, X, S, consts, resid=cur)
                cur = X
            if "ffn2" in phases:
                ffn_phase(k, ps, cur, X, gam_all[3 * l + 2], wts[(l, "ffn2", "g")], wts[(l, "ffn2", "u")],
                          wts[(l, "ffn2", "d")], S, consts)
                cur = X
        final_norm_phase(k, ps, cur, yT, gam_all[3 * depth], S, consts)
    return nc


def _gam_layout(g):
    return np.ascontiguousarray(g.reshape(NDC, 128).T)


def _w_in_layout(w, nout):
    return np.ascontiguousarray(w.reshape(NDC, 128, nout, 128).transpose(2, 1, 0, 3).reshape(nout, 128, D))


def _w_gu_layout(w):
    return _w_in_layout(w, NFC)


def _w_d_layout(w):
    return np.ascontiguousarray(w.reshape(NFC, 128, NDC, 128).transpose(2, 1, 0, 3).reshape(NDC, 128, NFC * 128))


def _consts(S):
    ident = np.eye(128, dtype=np.float32)
    ii = np.arange(128)
    masku = (ii[:, None] <= ii[None, :]).astype(np.float32)
    maskl = (ii[:, None] >= ii[None, :]).astype(np.float32)
    esel = np.zeros((128, 1024), np.float32)
    for h in range(8):
        esel[h, h * 128:(h + 1) * 128] = 1.0
    cst = np.concatenate([ident, masku, maskl, esel], axis=1)
    inv = (1.0 / (np.float32(10000.0) ** (np.arange(0, 128, 2, dtype=np.float32) / np.float32(128)))).astype(np.float32)
    ang = (np.arange(S, dtype=np.float32)[:, None] * inv[None, :]).astype(np.float32)
    emb = np.concatenate([ang, ang], axis=-1)
    cos = np.cos(emb).astype(np.float32).T
    sin = np.sin(emb).astype(np.float32).T.copy()
    sgn = np.ones((128, 1), np.float32)
    sgn[:64] = -1.0
    return {"cst": np.ascontiguousarray(cst), "ropec": np.ascontiguousarray(cos), "ropes": np.ascontiguousarray(sin * sgn)}


def make_shared_inputs(inp, depth=DEPTH, S=SEQ):
    sh = dict(_consts(S))
    gl = []
    for l in range(depth):
        gl += [_gam_layout(inp["ffn1_norm"][l]), _gam_layout(inp["mix_norm"][l]), _gam_layout(inp["ffn2_norm"][l])]
    gl.append(_gam_layout(inp["final_norm"]))
    sh["gam_all"] = np.stack(gl).astype(np.float32)
    for l in range(depth):
        for nm in ("ffn1", "ffn2"):
            sh[f"{nm}_wg{l}"] = _w_gu_layout(inp[f"{nm}_w_gate"][l])
            sh[f"{nm}_wu{l}"] = _w_gu_layout(inp[f"{nm}_w_up"][l])
            sh[f"{nm}_wd{l}"] = _w_d_layout(inp[f"{nm}_w_down"][l])
        w = inp["w_in"][l]
        wr = np.concatenate([w[:, 0:4096], w[:, 4128:7200], w[:, 4096:4128], np.zeros((D, 96), np.float32)], axis=1)
        sh[f"win{l}"] = _w_in_layout(wr, NZC)
        sh[f"wout{l}"] = _w_in_layout(inp["w_out"][l], NDC)
        sh[f"convw{l}"] = np.ascontiguousarray(inp["conv_qk"][l].reshape(3, NDC, 128).transpose(2, 1, 0).reshape(128, 48))
        sh[f"bgate{l}"] = np.ascontiguousarray(inp["b_gate"][l].T)
        sh[f"mnorm{l}"] = np.ascontiguousarray(inp["mlstm_norm"][l].reshape(8, 128).T)
        lamrow = np.concatenate([inp["lambda_q1"][l], inp["lambda_k1"][l], inp["lambda_q2"][l], inp["lambda_k2"][l]])
        sh[f"lamv{l}"] = np.ascontiguousarray(np.tile(lamrow[None, :], (128, 1)))
        sh[f"subln{l}"] = np.ascontiguousarray(inp["diff_subln"][l].reshape(2, 128).T)
    return {k_: np.ascontiguousarray(v, dtype=np.float32) for k_, v in sh.items()}


def kernel(**inputs):
    inp = {k_: np.asarray(v) for k_, v in inputs.items()}
    seqs = [inp["x_prompt"][i] for i in range(inp["x_prompt"].shape[0])] + \
           [inp["x_sample"][i] for i in range(inp["x_sample"].shape[0])]
    sh = make_shared_inputs(inp)
    nc = build_program()
    core_of_seq = [0, 1, 4, 5, 6, 7]
    zero_map = {k_: np.zeros_like(v) for k_, v in sh.items() if k_ not in ("cst", "ropec", "ropes")}
    zero_map.update({k_: sh[k_] for k_ in ("cst", "ropec", "ropes")})
    zero_map["xT"] = np.zeros((D, SEQ), np.float32)
    in_maps = [None] * N_CORES
    for si, c in enumerate(core_of_seq):
        m = dict(sh)
        m["xT"] = np.ascontiguousarray(seqs[si].T)
        in_maps[c] = m
    for c in range(N_CORES):
        if in_maps[c] is None:
            in_maps[c] = zero_map
    res = run_bass_kernel_spmd(nc, in_maps, core_ids=list(range(N_CORES)))
    outs = [np.ascontiguousarray(res.results[c]["yT"].T) for c in core_of_seq]
    nb = inp["x_prompt"].shape[0]
    y_prompt = np.stack(outs[:nb]).astype(np.float32)
    y_sample = np.stack(outs[nb:]).astype(np.float32)
    return (y_prompt, y_sample)
```
